# Optimizing a Trainium2 kernel written in Bass

```python
import math
import jax, jax.numpy as jnp
from jax import lax
import numpy as np

D_MODEL = 1024
BATCH = 4
SEQ = 4096
DEPTH = 1

CHUNK = 64
N_HEADS = 16
N_KV_HEADS = 4
HEAD_DIM = 64
WINDOW = 128
WINDOW_CHUNKS = WINDOW // CHUNK
N_BUCKETS = 32
MAX_DISTANCE = 128
LRU_WIDTH = D_MODEL
LRU_BLOCKS = 16
LRU_BLOCK = LRU_WIDTH // LRU_BLOCKS
CONV_WIDTH = 4
LRU_C = 8.0
D_FF = 2816
RMS_EPS = 1e-6
Q_W = N_HEADS * HEAD_DIM
KV_W = N_KV_HEADS * HEAD_DIM
IN_W = Q_W + 2 * KV_W + 2 * LRU_WIDTH
N_BRANCHES = 2
NEG_INF = -1e30

kernel_name = "hybrid_rglru_swa_sink_macaron"


def rms_norm(x, g):
    xf = x.astype(jnp.float32)
    y = xf * lax.rsqrt(jnp.mean(xf * xf, axis=-1, keepdims=True) + RMS_EPS)
    return (y * g.astype(jnp.float32)).astype(x.dtype)


def swiglu(x, w1, w3, w2):
    return (jax.nn.silu(x @ w1) * (x @ w3)) @ w2


def t5_bucket(rel):
    nb = N_BUCKETS // 2
    max_exact = nb // 2
    ret = jnp.where(rel > 0, nb, 0)
    n = jnp.abs(rel)
    nf = jnp.maximum(n, 1).astype(jnp.float32)
    large = max_exact + (jnp.log(nf / max_exact) / math.log(MAX_DISTANCE / max_exact)
                         * (nb - max_exact)).astype(jnp.int32)
    large = jnp.minimum(large, nb - 1)
    return ret + jnp.where(n < max_exact, n, large)


def band_rel_bias(table):
    kb = (WINDOW_CHUNKS + 1) * CHUNK
    i = jnp.arange(CHUNK)[:, None]
    j = jnp.arange(kb)[None, :]
    buckets = t5_bucket(j - WINDOW_CHUNKS * CHUNK - i)
    return jnp.transpose(table[buckets], (2, 0, 1))


def causal_conv(x, w, b):
    s = x.shape[1]
    xp = jnp.pad(x, ((0, 0), (CONV_WIDTH - 1, 0), (0, 0)))
    y = b
    for tap in range(CONV_WIDTH):
        y = y + xp[:, tap:tap + s] * w[tap]
    return y


def rg_lru(x, wa, ba, wx, bx, lam):
    b, s, _ = x.shape
    xf = x.astype(jnp.float32)
    xb = xf.reshape(b, s, LRU_BLOCKS, LRU_BLOCK)
    r = jax.nn.sigmoid(jnp.einsum('bshi,hij->bshj', xb, wa.astype(jnp.float32)).reshape(b, s, LRU_WIDTH) + ba)
    ig = jax.nn.sigmoid(jnp.einsum('bshi,hij->bshj', xb, wx.astype(jnp.float32)).reshape(b, s, LRU_WIDTH) + bx)
    log_a = -LRU_C * r * jax.nn.softplus(-lam.astype(jnp.float32))
    a = jnp.exp(log_a)
    u = jnp.sqrt(-jnp.expm1(2.0 * log_a)) * (ig * xf)

    def combine(c1, c2):
        a1, b1 = c1
        a2, b2 = c2
        return a1 * a2, a2 * b1 + b2

    _, h = lax.associative_scan(combine, (a, u), axis=1)
    return h.astype(x.dtype)


def swa_sink_attention(q, k, v, sinks, bias):
    b, s, _ = q.shape
    nc = s // CHUNK
    kb = (WINDOW_CHUNKS + 1) * CHUNK
    rep = N_HEADS // N_KV_HEADS
    qc = q.reshape(b, nc, CHUNK, N_KV_HEADS, rep, HEAD_DIM)
    pad = ((0, 0), (WINDOW_CHUNKS * CHUNK, 0), (0, 0))
    kp = jnp.pad(k, pad).reshape(b, nc + WINDOW_CHUNKS, CHUNK, N_KV_HEADS, HEAD_DIM)
    vp = jnp.pad(v, pad).reshape(b, nc + WINDOW_CHUNKS, CHUNK, N_KV_HEADS, HEAD_DIM)
    kband = jnp.concatenate([kp[:, w:w + nc] for w in range(WINDOW_CHUNKS + 1)], axis=2)
    vband = jnp.concatenate([vp[:, w:w + nc] for w in range(WINDOW_CHUNKS + 1)], axis=2)
    sc = jnp.einsum('bnqgrd,bnkgd->bngrqk', qc, kband).astype(jnp.float32) * (HEAD_DIM ** -0.5)
    sc = sc + bias.astype(jnp.float32).reshape(N_KV_HEADS, rep, CHUNK, kb)
    key_chunk = jnp.arange(nc)[:, None] - WINDOW_CHUNKS + jnp.arange(kb)[None, :] // CHUNK
    valid = key_chunk >= 0
    sc = jnp.where(valid[None, :, None, None, None, :], sc, NEG_INF)
    sink = jnp.broadcast_to(sinks.astype(jnp.float32).reshape(1, 1, N_KV_HEADS, rep, 1, 1),
                            sc.shape[:-1] + (1,))
    p = jax.nn.softmax(jnp.concatenate([sc, sink], axis=-1), axis=-1)[..., :-1]
    o = jnp.einsum('bngrqk,bnkgd->bnqgrd', p.astype(vband.dtype), vband)
    return o.reshape(b, s, Q_W)


def setup_inputs(seed: int = 0) -> dict:
    key = jax.random.key(seed)
    ks = jax.random.split(key, 32)
    f32 = jnp.float32
    L, D = DEPTH, D_MODEL

    def nrm(k, shape, scale):
        return jax.random.normal(k, shape, f32) * scale

    def gain(k):
        return 1.0 + 0.05 * jax.random.normal(k, (L, D), f32)

    a0 = jax.random.uniform(ks[10], (L, LRU_WIDTH), f32, 0.9, 0.999)
    return {
        "x": jax.random.normal(ks[0], (BATCH, SEQ, D), f32),
        "ffn1_pre_g": gain(ks[1]),
        "ffn1_w1": nrm(ks[2], (L, D, D_FF), D ** -0.5),
        "ffn1_w3": nrm(ks[3], (L, D, D_FF), D ** -0.5),
        "ffn1_w2": nrm(ks[4], (L, D_FF, D), D_FF ** -0.5),
        "ffn1_post_g": gain(ks[5]),
        "mix_pre_g": gain(ks[6]),
        "w_in": nrm(ks[7], (L, D, IN_W), D ** -0.5),
        "conv_w": nrm(ks[8], (L, CONV_WIDTH, LRU_WIDTH), CONV_WIDTH ** -0.5),
        "conv_b": nrm(ks[9], (L, LRU_WIDTH), 0.02),
        "rg_a_w": nrm(ks[11], (L, LRU_BLOCKS, LRU_BLOCK, LRU_BLOCK), LRU_BLOCK ** -0.5),
        "rg_a_b": nrm(ks[12], (L, LRU_WIDTH), 0.1),
        "rg_x_w": nrm(ks[13], (L, LRU_BLOCKS, LRU_BLOCK, LRU_BLOCK), LRU_BLOCK ** -0.5),
        "rg_x_b": nrm(ks[14], (L, LRU_WIDTH), 0.1),
        "lru_lambda": jnp.log(a0) - jnp.log1p(-a0),
        "w_lru_out": nrm(ks[15], (L, LRU_WIDTH, D), LRU_WIDTH ** -0.5),
        "attn_sinks": nrm(ks[16], (L, N_HEADS), 0.5),
        "rel_bias": nrm(ks[17], (N_BUCKETS, N_HEADS), 0.5),
        "w_attn_out": nrm(ks[18], (L, Q_W, D), Q_W ** -0.5),
        "w_gate": nrm(ks[19], (L, D, N_BRANCHES * D), D ** -0.5),
        "b_gate": nrm(ks[20], (L, N_BRANCHES * D), 0.1),
        "w_o": nrm(ks[21], (L, D, D), D ** -0.5),
        "mix_post_g": gain(ks[22]),
        "ffn2_pre_g": gain(ks[23]),
        "ffn2_w1": nrm(ks[24], (L, D, D_FF), D ** -0.5),
        "ffn2_w3": nrm(ks[25], (L, D, D_FF), D ** -0.5),
        "ffn2_w2": nrm(ks[26], (L, D_FF, D), D_FF ** -0.5),
        "ffn2_post_g": gain(ks[27]),
    }


def reference(x, ffn1_pre_g, ffn1_w1, ffn1_w3, ffn1_w2, ffn1_post_g, mix_pre_g, w_in, conv_w, conv_b,
              rg_a_w, rg_a_b, rg_x_w, rg_x_b, lru_lambda, w_lru_out, attn_sinks, rel_bias, w_attn_out,
              w_gate, b_gate, w_o, mix_post_g, ffn2_pre_g, ffn2_w1, ffn2_w3, ffn2_w2, ffn2_post_g):
    b, s, d = x.shape
    bias = band_rel_bias(rel_bias)
    splits = [Q_W, Q_W + KV_W, Q_W + 2 * KV_W, Q_W + 2 * KV_W + LRU_WIDTH]
    h = x
    for l in range(DEPTH):
        f = swiglu(rms_norm(h, ffn1_pre_g[l]), ffn1_w1[l], ffn1_w3[l], ffn1_w2[l])
        h = h + 0.5 * rms_norm(f, ffn1_post_g[l])
        u = rms_norm(h, mix_pre_g[l])
        q, k, v, xr, xg = jnp.split(u @ w_in[l], splits, axis=-1)
        xr = causal_conv(xr, conv_w[l], conv_b[l])
        hr = rg_lru(xr, rg_a_w[l], rg_a_b[l], rg_x_w[l], rg_x_b[l], lru_lambda[l])
        y_a = (hr * jax.nn.gelu(xg)) @ w_lru_out[l]
        y_b = swa_sink_attention(q, k, v, attn_sinks[l], bias) @ w_attn_out[l]
        g = jax.nn.sigmoid(u @ w_gate[l] + b_gate[l]).reshape(b, s, N_BRANCHES, d)
        merged = g[:, :, 0] * y_a + g[:, :, 1] * y_b
        h = h + rms_norm(merged @ w_o[l], mix_post_g[l])
        f = swiglu(rms_norm(h, ffn2_pre_g[l]), ffn2_w1[l], ffn2_w3[l], ffn2_w2[l])
        h = h + 0.5 * rms_norm(f, ffn2_post_g[l])
    return h
```

```python
import math
from contextlib import ExitStack

import numpy as np
import concourse.bass as bass
import concourse.mybir as mybir
from concourse.bass_utils import run_bass_kernel_spmd

F32 = mybir.dt.float32
BF16 = mybir.dt.bfloat16
AF = mybir.ActivationFunctionType
ALU = mybir.AluOpType

D = 1024
DFF = 2816
NKC = 8
NFC = 22
NOWN = 2048
HALO = 128
NG = NOWN + HALO
EPS = 1e-6
NEG = -30000.0
STS = [(0, 640), (640, 1152), (1152, 1664), (1664, 2176)]
NPREV = 2048
STS_A = [(k * 512, (k + 1) * 512) for k in range(8)]
A2B = NPREV - HALO
SMAX = 640
NSLOT = 5
import os
DEBUG_NOCC = bool(int(os.environ.get("KDEBUG_NOCC", "0")))
DEBUG_STOP = int(os.environ.get("KDEBUG_STOP", "0"))


class _Stop(Exception):
    pass


_DEAD = [False]


_CKN = {}
DEBUG_N = int(os.environ.get("KDEBUG_N", "1"))
DEBUG_CB = int(os.environ.get("KDEBUG_CB", "99"))
DEBUG_NOPOOL = bool(int(os.environ.get("KDEBUG_NOPOOL", "0")))
DEBUG_SMALL = int(os.environ.get("KDEBUG_SMALL", "0"))
DEBUG_REV = bool(int(os.environ.get("KDEBUG_REV", "0")))
SMALL_NAMES = ("ffn2_w1", "ffn2_w3", "ffn2_w2", "w_in", "w_lru_out", "w_attn_out", "w_gate", "w_o") if DEBUG_SMALL == 1 else ("ffn2_w1", "ffn2_w3", "ffn2_w2", "w_lru_out", "w_attn_out", "w_gate", "w_o")


def ck(n):
    if DEBUG_STOP == n:
        _CKN[n] = _CKN.get(n, 0) + 1
        if _CKN[n] == DEBUG_N:
            _DEAD[0] = True
EH = [0, 1, 2, 3, 8, 9, 10, 11]
OH = [4, 5, 6, 7, 12, 13, 14, 15]

VEC_NAMES = ["ffn1_pre_g", "ffn1_post_g", "mix_pre_g", "conv_w0", "conv_w1", "conv_w2", "conv_w3",
             "conv_b", "rg_a_b", "rg_x_b", "lru_lambda", "b_gate_a", "b_gate_b", "mix_post_g",
             "ffn2_pre_g", "ffn2_post_g"]
VIDX = {n: i * 8 for i, n in enumerate(VEC_NAMES)}
NV = len(VEC_NAMES) * 8


def subtiles(c0, c1):
    if c1 - c0 <= 512:
        return [(c0, c1)]
    out = []
    c = c0
    while c < c1:
        w = min(384, c1 - c)
        out.append((c, c + w))
        c += w
    return out


def kk(buf, kc, c0, c1):
    return [f"{buf}:{kc}:{b}" for b in range(c0 // 128, (c1 + 127) // 128)]


def kall(buf, c0, c1, n=NKC):
    r = []
    for kc in range(n):
        r += kk(buf, kc, c0, c1)
    return r


class Prog:
    def __init__(self, nc, es):
        self.nc = nc
        self.es = es
        self.engs = {"pe": nc.tensor, "act": nc.scalar, "dve": nc.vector, "pool": nc.gpsimd, "sp": nc.sync}
        self.toks = []
        self.last_w = {}
        self.readers = {}
        self.bar = {}
        self.semh = {}
        self.cnt = {}
        self.known = {e: {} for e in self.engs}
        self.nwait = 0
        self.last_tok = {}
        self.maxwait = {}
        self.simq = {}
        self.ewaits = {}

    def sem(self, name):
        if name not in self.semh:
            self.semh[name] = self.es.enter_context(self.nc.semaphore("s_" + name.replace(":", "_")))
            self.cnt[name] = 0
        return self.semh[name]

    def op(self, eng, fn, r=(), w=(), dma=None, sig=True):
        if _DEAD[0]:
            return -1
        i = len(self.toks)
        deps = set(self.bar.values())
        for k in r:
            if k in self.last_w:
                deps.add(self.last_w[k])
        for k in w:
            if k in self.last_w:
                deps.add(self.last_w[k])
            rd = self.readers.get(k)
            if rd:
                deps.update(rd.values())
        tokname = ("dma:" + dma) if dma else eng
        for k in r:
            self.readers.setdefault(k, {})[tokname] = i
        for k in w:
            self.last_w[k] = i
            self.readers[k] = {}
        h = self.engs[eng]
        need = {}
        for d in deps:
            s_, v_, e_, isd = self.toks[d]
            if (not isd) and e_ == "pe" and eng == "pe" and dma is None:
                continue
            if v_ > need.get(s_, 0):
                need[s_] = v_
        kn = self.known[eng]
        self.simq.setdefault(eng, []).append((dict(need), tokname, 16 if dma is not None else (1 if sig else 0)))
        for s_, v_ in need.items():
            if kn.get(s_, 0) >= v_:
                continue
            h.wait_ge(self.sem(s_), v_)
            kn[s_] = v_
            self.nwait += 1
            self.maxwait[s_] = max(self.maxwait.get(s_, 0), v_)
            self.ewaits[eng] = self.ewaits.get(eng, 0) + 1
        ins = fn(h)
        sh = self.sem(tokname)
        if dma is not None:
            self.cnt[tokname] += 16
            ins.then_inc(sh, 16)
            self.toks.append((tokname, self.cnt[tokname], eng, True))
        elif sig:
            self.cnt[tokname] += 1
            ins.then_inc(sh, 1)
            self.toks.append((tokname, self.cnt[tokname], eng, False))
        else:
            self.toks.append((tokname, self.cnt[tokname] + 1, eng, False))
        self.last_tok[tokname] = i
        return i

    def barrier(self):
        self.bar = dict(self.last_tok)

    def simulate(self):
        cnt = {}
        pos = {e: 0 for e in self.simq}
        progress = True
        while progress:
            progress = False
            for e, q in self.simq.items():
                while pos[e] < len(q):
                    need, tok, inc = q[pos[e]]
                    if all(cnt.get(s_, 0) >= v_ for s_, v_ in need.items()):
                        cnt[tok] = cnt.get(tok, 0) + inc
                        pos[e] += 1
                        progress = True
                    else:
                        break
        stuck = {e: (pos[e], len(q), q[pos[e]][0]) for e, q in self.simq.items() if pos[e] < len(q)}
        if stuck:
            print("[kernel] DEADLOCK in simulation:", {e: (a, b, {k: (v, cnt.get(k, 0)) for k, v in n.items()}) for e, (a, b, n) in stuck.items()})
        return not stuck

    def emit(self):
        self.simulate()
        print("[kernel] per-engine ops:", {e: len(q) for e, q in self.simq.items()}, "waits:", self.ewaits)
        print("[kernel] sem counts:", {k: v for k, v in self.cnt.items() if v > 0})
        for s_, v_ in self.maxwait.items():
            if v_ > self.cnt.get(s_, 0):
                print(f"[kernel] WARNING: wait {s_}>={v_} but final count {self.cnt.get(s_, 0)}")
        for s_, v_ in self.cnt.items():
            if s_.startswith("dma:") and v_ > 0:
                self.nc.sync.wait_ge(self.semh[s_], v_)
        return len(self.toks), self.nwait


def build_program():
    nc = bass.Bass("TRN2", target_bir_lowering=False)

    def din(name, shape, dt=F32):
        if DEBUG_SMALL and name in SMALL_NAMES:
            shape = [128, 128]
        return nc.dram_tensor(name, list(shape), dt, kind="ExternalInput").ap()

    xs = din("xs", [NPREV + NOWN, D])
    flag_d = din("flag", [128, 1])
    w1a = din("ffn1_w1", [D, DFF]); w3a = din("ffn1_w3", [D, DFF]); w2a = din("ffn1_w2", [DFF, D])
    w1b = din("ffn2_w1", [D, DFF]); w3b = din("ffn2_w3", [D, DFF]); w2b = din("ffn2_w2", [DFF, D])
    w_in = din("w_in", [D, 3584])
    w_lo = din("w_lru_out", [D, D]); w_ao = din("w_attn_out", [D, D])
    w_gate = din("w_gate", [D, 2 * D]); w_o = din("w_o", [D, D])
    vecs_d = din("vecs", [128, NV])
    rgbd_d = din("rgbd", [128, 2 * 8 * 128])
    biasT_d = din("biasT", [128, 16 * 2 * 128])
    hmask_d = din("hmask", [128, 1])
    sinks_d = din("sinks_b", [128, 16])
    ident_d = din("ident", [128, 128])
    sel_d = din("sel", [128, 8])
    y_out = nc.dram_tensor("y", [NOWN, D], F32, kind="ExternalOutput").ap()
    h_scr = nc.dram_tensor("h_scr", [128, NKC, NG], F32).ap()
    a_scr = nc.dram_tensor("a_scr", [128, NKC, NOWN], F32).ap()
    b_scr = nc.dram_tensor("b_scr", [128, NKC, NOWN], F32).ap()
    cc_in = nc.dram_tensor("cc_in", [128, 8], F32).ap()
    cc_out = nc.dram_tensor("cc_out", [128 * 8, 8], F32).ap()

    with ExitStack() as es:
        P = Prog(nc, es)

        state = {"bank": 0, "wn": 0, "sn": 0}

        def sb(name, shape, dt, stack=es):
            state["nm"] = state.get("nm", 0) + 1
            return stack.enter_context(nc.sbuf_tensor(f"sb{state['nm']}_{name}", list(shape), dt))

        vecs = sb("vecs", [128, NV], F32)
        dvec = sb("dvec", [128, 40], F32)
        identF = sb("identF", [128, 128], F32)
        identB = sb("identB", [128, 128], BF16)
        onesB = sb("onesB", [128, 128], BF16)
        rgbd = sb("rgbd", [128, 2, 8, 128], BF16)
        ring = sb("ring", [128, NSLOT, 4096], BF16)
        stage = sb("stage", [128, 2, 2048], F32)
        rstd = sb("rstd", [128, SMAX], F32)
        carry = sb("carry", [128, 8], F32)
        carry2 = sb("carry2", [128, 8], F32)
        tailsave = sb("tailsave", [128, 8, 3], F32)
        gsb = sb("gsb", [128, 8, 8], F32)
        selb = sb("selb", [128, 8], F32)
        hmask = sb("hmask", [128, 1], F32)
        flagb = sb("flagb", [128, 1], F32)
        psf = es.enter_context(nc.psum_tensor("psf", [128, 7, 512], F32))
        psb = es.enter_context(nc.psum_tensor("psb", [128, 1024], BF16))

        def bank():
            b = state["bank"] % 7
            state["bank"] += 1
            return b

        def V(name, kc=None):
            i = VIDX[name]
            if kc is None:
                return vecs[:, i:i + 8]
            return vecs[:, i + kc:i + kc + 1]

        P.op("sp", lambda e: e.dma_start(out=vecs[:], in_=vecs_d[:, :]), w=["vecs"], dma="c0")
        P.op("sp", lambda e: e.dma_start(out=identF[:], in_=ident_d[:, :]), w=["identF"], dma="c1")
        P.op("sp", lambda e: e.dma_start(out=hmask[:], in_=hmask_d[:, :]), w=["hmask"], dma="c4")
        P.op("sp", lambda e: e.dma_start(out=flagb[:], in_=flag_d[:, :]), w=["flagb"], dma="c5")
        for gi_ in range(2):
            P.op("sp", lambda e, gi_=gi_: e.dma_start(out=stage[:, gi_, 0:1024], in_=rgbd_d[:, gi_ * 1024:(gi_ + 1) * 1024]),
                 w=[f"stage{gi_}"], dma=f"stg{gi_}")
            P.op("dve", lambda e, gi_=gi_: e.tensor_copy(out=rgbd[:, gi_, :, :].rearrange("p b c -> p (b c)"), in_=stage[:, gi_, 0:1024]),
                 r=[f"stage{gi_}"], w=["rgbd"])
        P.op("sp", lambda e: e.dma_start(out=selb[:], in_=sel_d[:, :]), w=["selb"], dma="c3")
        P.op("dve", lambda e: e.tensor_copy(out=identB[:], in_=identF[:]), r=["identF"], w=["identB"])
        P.op("dve", lambda e: e.memset(onesB[:], 1.0), w=["onesB"])
        P.op("dve", lambda e: e.memset(dvec[:, 32:33], EPS), w=["dv_c"])
        P.op("dve", lambda e: e.memset(dvec[:, 33:34], 1.0), r=["dv_c"], w=["dv_c"])
        P.op("dve", lambda e: e.memset(dvec[:, 34:35], 0.0), r=["dv_c"], w=["dv_c"])
        P.op("dve", lambda e: e.tensor_scalar(out=dvec[:, 0:8], in0=V("ffn1_post_g"), scalar1=0.5, scalar2=None,
                                              op0=ALU.mult), r=["vecs"], w=["dv_g1"])
        P.op("dve", lambda e: e.tensor_scalar(out=dvec[:, 8:16], in0=V("ffn2_post_g"), scalar1=0.5, scalar2=None,
                                              op0=ALU.mult), r=["vecs"], w=["dv_g2"])
        P.op("act", lambda e: e.activation(out=dvec[:, 16:24], in_=V("lru_lambda"), func=AF.Exp, scale=-1.0),
             r=["vecs"], w=["dv_c1"])
        P.op("act", lambda e: e.activation(out=dvec[:, 16:24], in_=dvec[:, 16:24], func=AF.Ln, bias=dvec[:, 33:34]),
             r=["dv_c1", "dv_c"], w=["dv_c1"])
        P.op("dve", lambda e: e.tensor_scalar(out=dvec[:, 24:32], in0=dvec[:, 16:24], scalar1=-16.0, scalar2=None,
                                              op0=ALU.mult), r=["dv_c1"], w=["dv_c2"])
        P.op("dve", lambda e: e.tensor_scalar(out=dvec[:, 16:24], in0=dvec[:, 16:24], scalar1=-8.0, scalar2=None,
                                              op0=ALU.mult), r=["dv_c1", "dv_c2"], w=["dv_c1"])
        eps_ap = dvec[:, 32:33]
        one_ap = dvec[:, 33:34]

        def wload(wap, r0, nk, c0, W):
            slot = state["wn"] % NSLOT
            state["wn"] += 1
            if _DEAD[0]:
                return slot, ring[:, slot, 0:nk * W].rearrange("p (k n) -> p k n", n=W)
            halves = [(0, (nk + 1) // 2), ((nk + 1) // 2, nk)]
            for (k0, k1) in halves:
                if k1 <= k0:
                    continue
                st = state["sn"] % 2
                state["sn"] += 1
                n = (k1 - k0) * W
                src = wap[r0 + k0 * 128:r0 + k1 * 128, c0:c0 + W].rearrange("(k p) n -> p k n", p=128)
                dstv = stage[:, st, 0:n].rearrange("p (k n) -> p k n", n=W)
                P.op("sp", lambda e, dstv=dstv, src=src: e.dma_start(out=dstv, in_=src),
                     w=[f"stage{st}"], dma=f"stg{st}")
                ce = "act" if (state["sn"] % 2 == 0 or DEBUG_NOPOOL) else "pool"
                rv = ring[:, slot, k0 * W:k1 * W]
                sv = stage[:, st, 0:n]
                if ce == "act":
                    P.op("act", lambda e, rv=rv, sv=sv: e.activation(out=rv, in_=sv, func=AF.Copy),
                         r=[f"stage{st}"], w=[f"ring{slot}"])
                else:
                    P.op("pool", lambda e, rv=rv, sv=sv: e.tensor_copy(out=rv, in_=sv),
                         r=[f"stage{st}"], w=[f"ring{slot}"])
            return slot, ring[:, slot, 0:nk * W].rearrange("p (k n) -> p k n", n=W)

        class WQ:
            def __init__(self, blocks):
                self.blocks = blocks
                self.loaded = []

            def ensure(self, n):
                while len(self.loaded) <= min(n, len(self.blocks) - 1):
                    self.loaded.append(wload(*self.blocks[len(self.loaded)]))

            def get(self, i, lo=None):
                lo = i if lo is None else lo
                self.ensure(max(i, lo + NSLOT - 1))
                return self.loaded[i]

        def mm_group(out_ap, pairs, rkeys, bkey):
            n = len(pairs)
            for i, (l, r_) in enumerate(pairs):
                P.op("pe", lambda e, l=l, r_=r_, i=i: e.matmul(out_ap, lhsT=l, rhs=r_, start=(i == 0), stop=(i == n - 1)),
                     r=rkeys[i], w=[bkey], sig=(i == n - 1))

        def rmsnorm(src, sname, gname, dst, dname, c0, c1, sq_done=False, gvec=None):
            for (a, b) in subtiles(c0, c1):
                T = b - a
                if not sq_done:
                    P.op("act", lambda e, a=a, b=b: e.activation(out=dst[:, :, a:b], in_=src[:, :, a:b], func=AF.Square),
                         r=kall(sname, a, b), w=kall(dname, a, b))
                bk = bank()
                mm_group(psf[:, bk, 0:T], [(onesB[:], dst[:, kc, a:b]) for kc in range(NKC)],
                         [["onesB"] + kk(dname, kc, a, b) for kc in range(NKC)], f"ps{bk}")
                P.op("act", lambda e, a=a, b=b, bk=bk, T=T: e.activation(out=rstd[:, a:b], in_=psf[:, bk, 0:T], func=AF.Sqrt,
                                                                      scale=1.0 / D, bias=eps_ap),
                     r=[f"ps{bk}", "dv_c"], w=kk("rstd", 0, a, b))
                P.op("dve", lambda e, a=a, b=b: e.reciprocal(out=rstd[:, a:b], in_=rstd[:, a:b]),
                     r=kk("rstd", 0, a, b), w=kk("rstd", 0, a, b))
                if gname is not None:
                    for kc in range(NKC):
                        P.op("dve", lambda e, a=a, b=b, kc=kc: e.scalar_tensor_tensor(
                            out=dst[:, kc, a:b], in0=src[:, kc, a:b], scalar=V(gname, kc), in1=rstd[:, a:b],
                            op0=ALU.mult, op1=ALU.mult),
                            r=kk(sname, kc, a, b) + kk("rstd", 0, a, b) + ["vecs"], w=kk(dname, kc, a, b))

        def ffn(xT, nb, fT, h1, sg, w1, w3, w2, gpre, ghalf_off, c0, c1):
            rmsnorm(xT, "xT", gpre, nb, "nb", c0, c1)
            ck(21)
            subs = subtiles(c0, c1)
            blocks = []
            for cb in range(6):
                W = 512 if cb < 5 else 256
                blocks.append((w1, 0, 8, cb * 512, W))
                blocks.append((w3, 0, 8, cb * 512, W))
            for mb in range(2):
                blocks.append((w2, 0, 8, mb * 512, 512))
                blocks.append((w2, 1024, 8, mb * 512, 512))
                blocks.append((w2, 2048, 6, mb * 512, 512))
            wq = WQ(blocks)
            for cb in range(6):
                W = 512 if cb < 5 else 256
                s1, v1 = wq.get(2 * cb, 2 * cb)
                s3, v3 = wq.get(2 * cb + 1, 2 * cb)
                for jc in range(W // 128):
                    if cb >= DEBUG_CB:
                        continue
                    j = cb * 4 + jc
                    for si, (a, b) in enumerate(subs):
                        T = b - a
                        ba, bb = bank(), bank()
                        mm_group(psf[:, ba, 0:T], [(v1[:, kc, jc * 128:(jc + 1) * 128], nb[:, kc, a:b]) for kc in range(NKC)],
                                 [[f"ring{s1}"] + kk("nb", kc, a, b) for kc in range(NKC)], f"ps{ba}")
                        mm_group(psf[:, bb, 0:T], [(v3[:, kc, jc * 128:(jc + 1) * 128], nb[:, kc, a:b]) for kc in range(NKC)],
                                 [[f"ring{s3}"] + kk("nb", kc, a, b) for kc in range(NKC)], f"ps{bb}")
                        sgi = state.get("sgi", 0)
                        state["sgi"] = sgi + 1
                        sgv = sg[:, sgi % 2, 0:T]
                        P.op("act", lambda e, sgv=sgv, ba=ba, T=T: e.activation(out=sgv, in_=psf[:, ba, 0:T], func=AF.Silu),
                             r=[f"ps{ba}"], w=[f"sg{sgi % 2}"])
                        P.op("dve", lambda e, sgv=sgv, bb=bb, T=T, j=j, a=a, b=b: e.tensor_tensor(
                            out=h1[:, j, a:b], in0=sgv, in1=psf[:, bb, 0:T], op=ALU.mult),
                            r=[f"sg{sgi % 2}", f"ps{bb}"], w=kk("h1", j, a, b))
            ck(22)
            for mb in range(2):
                sl = [wq.get(12 + 3 * mb + q_, 12 + 3 * mb) for q_ in range(3)]
                for mc in ((3, 2, 1, 0) if DEBUG_REV else range(4)):
                    m = mb * 4 + mc
                    for (a, b) in subs:
                        T = b - a
                        bk = bank()
                        pairs, rk = [], []
                        for j in range(NFC):
                            s_, v_ = sl[j // 8]
                            pairs.append((v_[:, j % 8, mc * 128:(mc + 1) * 128], h1[:, j, a:b]))
                            rk.append([f"ring{s_}"] + kk("h1", j, a, b))
                        mm_group(psf[:, bk, 0:T], pairs, rk, f"ps{bk}")
                        ck(24)
                        P.op("dve", lambda e, m=m, a=a, b=b, bk=bk, T=T: e.tensor_copy(out=fT[:, m, a:b], in_=psf[:, bk, 0:T]),
                             r=[f"ps{bk}"], w=kk("fT", m, a, b))
                        P.op("act", lambda e, m=m, a=a, b=b: e.activation(out=nb[:, m, a:b], in_=fT[:, m, a:b], func=AF.Square),
                             r=kk("fT", m, a, b), w=kk("nb", m, a, b))
                        ck(25)
            ck(23)
            post_residual(xT, nb, fT, dvec, ghalf_off, c0, c1)

        def post_residual(xT, nb, fT, gbuf, goff, c0, c1):
            rmsnorm(fT, "fT", None, nb, "nb", c0, c1, sq_done=True)
            for (a, b) in subtiles(c0, c1):
                for kc in range(NKC):
                    P.op("dve", lambda e, a=a, b=b, kc=kc: e.tensor_tensor(out=fT[:, kc, a:b], in0=fT[:, kc, a:b],
                                                                           in1=rstd[:, a:b], op=ALU.mult),
                         r=kk("fT", kc, a, b) + kk("rstd", 0, a, b), w=kk("fT", kc, a, b))
                    P.op("dve", lambda e, a=a, b=b, kc=kc: e.scalar_tensor_tensor(
                        out=xT[:, kc, a:b], in0=fT[:, kc, a:b], scalar=gbuf[:, goff + kc:goff + kc + 1], in1=xT[:, kc, a:b],
                        op0=ALU.mult, op1=ALU.add),
                        r=kk("fT", kc, a, b) + kk("xT", kc, a, b) + ["dv_g1", "dv_g2", "vecs"], w=kk("xT", kc, a, b))

        try:
            ck(1)
            with ExitStack() as pa:
                xT = sb("xT", [128, NKC, SMAX], F32, pa)
                nb = sb("nb", [128, NKC, SMAX], BF16, pa)
                fbuf = sb("fbuf", [128, NKC, SMAX + 4], F32, pa)
                fT = fbuf[:, :, 4:4 + SMAX]
                xin = sb("xin", [128, 2, D], F32, pa)
                for sti, (g0, g1) in enumerate(STS_A):
                    S = g1 - g0
                    own0 = 0
                    with ExitStack() as s1:
                        h1 = sb("h1", [128, NFC, SMAX], BF16, s1)
                        sg = sb("sg", [128, 2, 512], F32, s1)
                        for t in range(S // 128):
                            xb = t % 2
                            P.op("sp", lambda e, xb=xb, t=t, g0=g0: e.dma_start(out=xin[:, xb, :], in_=xs[g0 + t * 128:g0 + (t + 1) * 128, :]),
                                 w=[f"xin{xb}"], dma=f"xin{xb}")
                            for hb in range(2):
                                bk = bank()
                                for q in range(4):
                                    kc = hb * 4 + q
                                    P.op("pe", lambda e, bk=bk, q=q, kc=kc, xb=xb: e.transpose(
                                        psf[:, bk, q * 128:(q + 1) * 128], xin[:, xb, kc * 128:(kc + 1) * 128], identF[:]),
                                        r=[f"xin{xb}", "identF"], w=[f"ps{bk}"])
                                eng = "act" if hb == 0 else "dve"
                                ov = xT[:, hb * 4:(hb + 1) * 4, t * 128:(t + 1) * 128]
                                iv = psf[:, bk, :].rearrange("p (q n) -> p q n", n=128)
                                wk = []
                                for kc in range(hb * 4, hb * 4 + 4):
                                    wk += kk("xT", kc, t * 128, (t + 1) * 128)
                                if eng == "act":
                                    P.op("act", lambda e, ov=ov, iv=iv: e.activation(out=ov, in_=iv, func=AF.Copy), r=[f"ps{bk}"], w=wk)
                                else:
                                    P.op("dve", lambda e, ov=ov, iv=iv: e.tensor_copy(out=ov, in_=iv), r=[f"ps{bk}"], w=wk)
                        if sti == 0:
                            ck(2)
                        ffn(xT, nb, fT, h1, sg, w1a, w3a, w2a, "ffn1_pre_g", 0, 0, S)
                    if sti == 0:
                        ck(3)
                    P.barrier()
                    for kc in range(NKC):
                        if g1 <= A2B:
                            continue
                        ca = max(g0, A2B) - g0
                        P.op("sp", lambda e, kc=kc, g0=g0, S=S, ca=ca: e.dma_start(out=h_scr[:, kc, g0 + ca - A2B:g0 + S - A2B], in_=xT[:, kc, ca:S]),
                             r=kk("xT", kc, 0, S), w=[f"hscr{kc}:{sti}"], dma=f"hst{kc}")
                    rmsnorm(xT, "xT", "mix_pre_g", nb, "nb", 0, S)
                    if sti == 0:
                        P.op("dve", lambda e: e.memset(fbuf[:, :, 0:4], 0.0), r=kall("fT", 0, 128), w=["xtail"])
                    else:
                        P.op("dve", lambda e: e.tensor_copy(out=fbuf[:, :, 1:4], in_=tailsave[:]), r=["tailsave"] + kall("fT", 0, 128),
                             w=["xtail"])
                    wq = WQ([(w_in, 0, 8, 1536, 512), (w_in, 0, 8, 2048, 512)])
                    subs = subtiles(0, S)
                    for blk in range(2):
                        s_, v_ = wq.get(blk)
                        for jc in range(4):
                            c = blk * 4 + jc
                            for (a, b) in subs:
                                T = b - a
                                bk = bank()
                                mm_group(psf[:, bk, 0:T], [(v_[:, kc, jc * 128:(jc + 1) * 128], nb[:, kc, a:b]) for kc in range(NKC)],
                                         [[f"ring{s_}"] + kk("nb", kc, a, b) for kc in range(NKC)], f"ps{bk}")
                                P.op("act", lambda e, c=c, a=a, b=b, bk=bk, T=T: e.activation(out=fbuf[:, c, 4 + a:4 + b], in_=psf[:, bk, 0:T],
                                                                                           func=AF.Copy),
                                     r=[f"ps{bk}"], w=kk("fT", c, a, b))
                    P.op("dve", lambda e, S=S: e.tensor_copy(out=tailsave[:], in_=fbuf[:, :, 4 + S - 3:4 + S]),
                         r=kall("fT", S - 128, S), w=["tailsave"])
                    if sti == 0:
                        ck(4)
                    with ExitStack() as s2:
                        tmp = sb("tmpA", [128, 7, SMAX], F32, s2)
                        xc = xT
                        for c in range(NKC):
                            rk = kk("fT", c, 0, S) + ["xtail", "vecs"]
                            P.op("dve", lambda e, c=c, S=S: e.tensor_scalar(out=xc[:, c, 0:S], in0=fbuf[:, c, 1:1 + S],
                                                                           scalar1=V("conv_w0", c), scalar2=V("conv_b", c),
                                                                           op0=ALU.mult, op1=ALU.add),
                                 r=rk, w=kk("xT", c, 0, S))
                            for tap in range(1, 4):
                                P.op("dve", lambda e, c=c, S=S, tap=tap: e.scalar_tensor_tensor(
                                    out=xc[:, c, 0:S], in0=fbuf[:, c, 1 + tap:1 + tap + S], scalar=V(f"conv_w{tap}", c),
                                    in1=xc[:, c, 0:S], op0=ALU.mult, op1=ALU.add),
                                    r=rk + kk("xT", c, 0, S), w=kk("xT", c, 0, S))
                            P.op("act", lambda e, c=c, S=S: e.activation(out=nb[:, c, 0:S], in_=xc[:, c, 0:S], func=AF.Copy),
                                 r=kk("xT", c, 0, S), w=kk("nb", c, 0, S))
                        if g0 == NPREV:
                            P.op("dve", lambda e: e.tensor_scalar(out=carry[:], in0=carry[:], scalar1=flagb[:, 0:1], scalar2=None, op0=ALU.mult),
                                 r=[f"carry{c_}" for c_ in range(8)] + ["flagb"], w=[f"carry{c_}" for c_ in range(8)])
                            P.op("dve", lambda e: e.tensor_copy(out=carry2[:], in_=carry[:]),
                                 r=[f"carry{c_}" for c_ in range(8)], w=["carry2"])
                        for c in range(NKC):
                            ti = c % 1
                            rT = tmp[:, 0, :]; igT = tmp[:, 1, :]; aT = tmp[:, 2 + (c % 2) * 2, :]; bT = tmp[:, 3 + (c % 2) * 2, :]
                            ab = c % 2
                            for (a, b) in subs:
                                T = b - a
                                b1, b2 = bank(), bank()
                                P.op("pe", lambda e, c=c, a=a, b=b, b1=b1, T=T: e.matmul(psf[:, b1, 0:T], lhsT=rgbd[:, 0, c, :], rhs=nb[:, c, a:b],
                                                                                     start=True, stop=True),
                                     r=["rgbd"] + kk("nb", c, a, b), w=[f"ps{b1}"])
                                P.op("pe", lambda e, c=c, a=a, b=b, b2=b2, T=T: e.matmul(psf[:, b2, 0:T], lhsT=rgbd[:, 1, c, :], rhs=nb[:, c, a:b],
                                                                                     start=True, stop=True),
                                     r=["rgbd"] + kk("nb", c, a, b), w=[f"ps{b2}"])
                                P.op("act", lambda e, c=c, a=a, b=b, b1=b1, T=T, rT=rT: e.activation(out=rT[:, a:b], in_=psf[:, b1, 0:T], func=AF.Sigmoid,
                                                                                              bias=V("rg_a_b", c)),
                                     r=[f"ps{b1}", "vecs"], w=kk("tr", 0, a, b))
                                P.op("act", lambda e, c=c, a=a, b=b, b2=b2, T=T, igT=igT: e.activation(out=igT[:, a:b], in_=psf[:, b2, 0:T], func=AF.Sigmoid,
                                                                                                bias=V("rg_x_b", c)),
                                     r=[f"ps{b2}", "vecs"], w=kk("tig", 0, a, b))
                            P.op("act", lambda e, c=c, S=S, aT=aT, rT=rT: e.activation(out=aT[:, 0:S], in_=rT[:, 0:S], func=AF.Exp,
                                                                                   scale=dvec[:, 16 + c:17 + c]),
                                 r=kk("tr", 0, 0, S) + ["dv_c1"], w=[f"ta{ab}"])
                            P.op("act", lambda e, c=c, S=S, bT=bT, rT=rT: e.activation(out=bT[:, 0:S], in_=rT[:, 0:S], func=AF.Exp,
                                                                                   scale=dvec[:, 24 + c:25 + c]),
                                 r=kk("tr", 0, 0, S) + ["dv_c2"], w=[f"tb{ab}"])
                            P.op("act", lambda e, S=S, bT=bT: e.activation(out=bT[:, 0:S], in_=bT[:, 0:S], func=AF.Sqrt, scale=-1.0, bias=one_ap),
                                 r=[f"tb{ab}", "dv_c"], w=[f"tb{ab}"])
                            P.op("dve", lambda e, S=S, bT=bT, igT=igT: e.tensor_tensor(out=bT[:, 0:S], in0=bT[:, 0:S], in1=igT[:, 0:S], op=ALU.mult),
                                 r=[f"tb{ab}"] + kk("tig", 0, 0, S), w=[f"tb{ab}"])
                            P.op("dve", lambda e, c=c, S=S, bT=bT: e.tensor_tensor(out=bT[:, 0:S], in0=bT[:, 0:S], in1=xc[:, c, 0:S], op=ALU.mult),
                                 r=[f"tb{ab}"] + kk("xT", c, 0, S), w=[f"tb{ab}"])
                            o0 = g0 - NPREV
                            n_own = S
                            if g0 >= NPREV:
                                P.op("sp", lambda e, c=c, aT=aT, o0=o0, n_own=n_own, own0=own0, S=S: e.dma_start(
                                    out=a_scr[:, c, o0:o0 + n_own], in_=aT[:, own0:S]), r=[f"ta{ab}"], w=[f"ascr{c}:{sti}"], dma=f"ast{ab}")
                                P.op("sp", lambda e, c=c, bT=bT, o0=o0, n_own=n_own, own0=own0, S=S: e.dma_start(
                                    out=b_scr[:, c, o0:o0 + n_own], in_=bT[:, own0:S]), r=[f"tb{ab}"], w=[f"bscr{c}:{sti}"], dma=f"bst{ab}")
                            hs = tmp[:, 6, :]
                            if sti == 0:
                                P.op("dve", lambda e, c=c, hs=hs, aT=aT, bT=bT, own0=own0, S=S: e.tensor_tensor_scan(
                                    out=hs[:, own0:S], data0=aT[:, own0:S], data1=bT[:, own0:S], initial=0.0, op0=ALU.mult, op1=ALU.add),
                                    r=[f"ta{ab}", f"tb{ab}"], w=["ths"])
                            else:
                                P.op("dve", lambda e, c=c, hs=hs, aT=aT, bT=bT, own0=own0, S=S: e.tensor_tensor_scan(
                                    out=hs[:, own0:S], data0=aT[:, own0:S], data1=bT[:, own0:S], initial=carry[:, c:c + 1],
                                    op0=ALU.mult, op1=ALU.add),
                                    r=[f"ta{ab}", f"tb{ab}", f"carry{c}"], w=["ths"])
                            P.op("dve", lambda e, c=c, hs=hs, S=S: e.tensor_copy(out=carry[:, c:c + 1], in_=hs[:, S - 1:S]),
                                 r=["ths"], w=[f"carry{c}"])
                    P.barrier()

            ck(6)
            P.barrier()

            ck(7)
            with ExitStack() as pb:
                xT = sb("hT", [128, NKC, SMAX], F32, pb)
                nb = sb("nbB", [128, NKC, SMAX], BF16, pb)
                fbuf = sb("fbufB", [128, NKC, SMAX], F32, pb)
                fT = fbuf
                qT = fbuf[:, 0:4, :].rearrange("p a b -> p (a b)").bitcast(BF16).rearrange("p (k n) -> p k n", n=SMAX)
                oT = fbuf[:, 4:8, :].rearrange("p a b -> p (a b)").bitcast(BF16).rearrange("p (k n) -> p k n", n=SMAX)
                kT = sb("kT", [128, 2, NG], BF16, pb)
                Vaug = sb("Vaug", [128, NG // 128, 4, 65], BF16, pb)
                biasT = sb("biasT", [128, 16, 2, 128], F32, pb)
                esink = sb("esink", [128, 16], F32, pb)
                P.op("sp", lambda e: e.dma_start(out=biasT[:].rearrange("p a b c -> p (a b c)"), in_=biasT_d[:, :]), w=["biasT"], dma="c0")
                P.op("sp", lambda e: e.dma_start(out=esink[:], in_=sinks_d[:, :]), w=["esink"], dma="c2")
                P.op("act", lambda e: e.activation(out=esink[:], in_=esink[:], func=AF.Exp), r=["esink"], w=["esink"])
                P.op("dve", lambda e: e.memset(Vaug[:].rearrange("p a b c -> p (a b c)"), 1.0), w=["Vaug_all"])

                for sti, (g0, g1) in enumerate(STS):
                    S = g1 - g0
                    own0 = HALO if sti == 0 else 0
                    So = S - own0
                    osubs = subtiles(own0, S)
                    with ExitStack() as s1:
                        mg = sb("mg", [128, NKC, SMAX], BF16, s1)
                        att = sb("att", [128, 2, 512], F32, s1)
                        pT = sb("pT", [128, 2, 512], BF16, s1)
                        otok = sb("otok", [128, D], BF16, s1)
                        den = sb("den", [128, 4], F32, s1)
                        abuf = sb("abuf", [128, 2, SMAX], F32, s1)
                        bbuf = sb("bbuf", [128, 2, SMAX], F32, s1)
                        hrt = sb("hrt", [128, SMAX], F32, s1)
                        gt = sb("gt", [128, 2, 2, 512], F32, s1)
                        for kc in range(NKC):
                            P.op("sp", lambda e, kc=kc, g0=g0, S=S: e.dma_start(out=xT[:, kc, 0:S], in_=h_scr[:, kc, g0:g0 + S]),
                                 r=[f"hscr{kc}:{sti}"], w=kk("xT", kc, 0, S), dma=f"hld{kc}")
                        rmsnorm(xT, "xT", "mix_pre_g", nb, "nb", 0, S)
                        blocks = [(w_in, 0, 8, 1024, 512), (w_in, 0, 8, 0, 512), (w_in, 0, 8, 512, 512),
                                  (w_in, 0, 8, 2560, 512), (w_in, 0, 8, 3072, 512)]
                        for mb in range(2):
                            blocks += [(w_gate, 0, 8, mb * 512, 512), (w_gate, 0, 8, 1024 + mb * 512, 512),
                                       (w_lo, 0, 8, mb * 512, 512), (w_ao, 0, 8, mb * 512, 512)]
                        blocks += [(w_o, 0, 8, 0, 512), (w_o, 0, 8, 512, 512)]
                        wq = WQ(blocks)
                        s_, v_ = wq.get(0)
                        for c2 in range(2):
                            for (a, b) in subtiles(0, S):
                                T = b - a
                                bk = bank()
                                mm_group(psf[:, bk, 0:T], [(v_[:, kc, c2 * 128:(c2 + 1) * 128], nb[:, kc, a:b]) for kc in range(NKC)],
                                         [[f"ring{s_}"] + kk("nb", kc, a, b) for kc in range(NKC)], f"ps{bk}")
                                P.op("act", lambda e, c2=c2, a=a, b=b, bk=bk, T=T, g0=g0: e.activation(out=kT[:, c2, g0 + a:g0 + b], in_=psf[:, bk, 0:T],
                                                                                                func=AF.Copy),
                                     r=[f"ps{bk}"], w=kk("kT", c2, g0 + a, g0 + b))
                        for t in range(S // 128):
                            bk = bank()
                            mm_group(psf[:, bk, 0:256], [(nb[:, kc, t * 128:(t + 1) * 128], v_[:, kc, 256:512]) for kc in range(NKC)],
                                     [[f"ring{s_}"] + kk("nb", kc, t * 128, (t + 1) * 128) for kc in range(NKC)], f"ps{bk}")
                            vb = g0 // 128 + t
                            P.op("dve", lambda e, vb=vb, bk=bk: e.tensor_copy(out=Vaug[:, vb, :, 0:64],
                                                                             in_=psf[:, bk, 0:256].rearrange("p (g d) -> p g d", d=64)),
                                 r=[f"ps{bk}", "Vaug_all"], w=[f"V{vb}"])
                        if sti == 0:
                            ck(8)
                        for blk in range(2):
                            s_, v_ = wq.get(1 + blk)
                            for e4 in range(4):
                                i = blk * 4 + e4
                                for (a, b) in osubs:
                                    T = b - a
                                    bk = bank()
                                    for half in range(2):
                                        col = half * 256 + e4 * 64
                                        for kc in range(NKC):
                                            P.op("pe", lambda e, half=half, col=col, kc=kc, a=a, b=b, bk=bk, T=T, v_=v_: e.matmul(
                                                psf[half * 64:(half + 1) * 64, bk, 0:T], lhsT=v_[:, kc, col:col + 64], rhs=nb[:, kc, a:b],
                                                start=(kc == 0), stop=(kc == NKC - 1)),
                                                r=[f"ring{s_}"] + kk("nb", kc, a, b), w=[f"ps{bk}"])
                                    P.op("act", lambda e, i=i, a=a, b=b, bk=bk, T=T: e.activation(out=qT[:, i, a:b], in_=psf[:, bk, 0:T],
                                                                                               func=AF.Copy, scale=0.125),
                                         r=[f"ps{bk}"], w=kk("qT", i, a, b))
                        if sti == 0:
                            ck(9)
                        for qb in range(So // 128):
                            q0 = own0 + qb * 128
                            G0 = g0 + q0
                            first = (G0 == HALO)
                            for g in range(4):
                                c2, pbs = g // 2, 64 * (g % 2)
                                sb_ = [bank(), bank()]
                                for kb in range(2):
                                    kc0 = G0 - 128 + kb * 128
                                    for r_ in range(4):
                                        h = 4 * g + r_
                                        i = EH.index(h) if h in EH else OH.index(h)
                                        P.op("pe", lambda e, kb=kb, r_=r_, i=i, c2=c2, pbs=pbs, kc0=kc0, q0=q0, bk=sb_[kb]: e.matmul(
                                            psf[:, bk, r_ * 128:(r_ + 1) * 128], lhsT=kT[pbs:pbs + 64, c2, kc0:kc0 + 128],
                                            rhs=qT[pbs:pbs + 64, i, q0:q0 + 128], start=True, stop=True),
                                            r=kk("kT", c2, kc0, kc0 + 128) + kk("qT", i, q0, q0 + 128), w=[f"ps{sb_[kb]}"])
                                    bv = biasT[:, 4 * g:4 * g + 4, kb, :]
                                    bkey = "biasT"
                                    P.op("dve", lambda e, kb=kb, bv=bv, bk=sb_[kb]: e.tensor_tensor(
                                        out=att[:, kb, :].rearrange("p (r q) -> p r q", q=128),
                                        in0=psf[:, bk, :].rearrange("p (r q) -> p r q", q=128), in1=bv, op=ALU.add),
                                        r=[f"ps{sb_[kb]}", bkey], w=[f"att{kb}"])
                                    if first and kb == 0:
                                        P.op("dve", lambda e, kb=kb: e.tensor_scalar(out=att[:, kb, :], in0=att[:, kb, :], scalar1=hmask[:, 0:1],
                                                                                    scalar2=None, op0=ALU.add),
                                             r=[f"att{kb}", "hmask"], w=[f"att{kb}"])
                                    P.op("act", lambda e, kb=kb: e.activation(out=pT[:, kb, :], in_=att[:, kb, :], func=AF.Exp),
                                         r=[f"att{kb}"], w=[f"pT{kb}"])
                                ob = bank()
                                for r_ in range(4):
                                    for kb in range(2):
                                        vb = G0 // 128 - 1 + kb
                                        P.op("pe", lambda e, r_=r_, kb=kb, vb=vb, g=g, ob=ob: e.matmul(
                                            psf[:, ob, r_ * 128:r_ * 128 + 65], lhsT=pT[:, kb, r_ * 128:(r_ + 1) * 128], rhs=Vaug[:, vb, g, :],
                                            start=(kb == 0), stop=(kb == 1)),
                                            r=[f"pT{kb}", f"V{vb}", "Vaug_all"], w=[f"ps{ob}"])
                                P.op("dve", lambda e, g=g, ob=ob: e.tensor_tensor(
                                    out=den[:, :], in0=psf[:, ob, :].rearrange("p (r q) -> p r q", q=128)[:, :, 64],
                                    in1=esink[:, 4 * g:4 * g + 4], op=ALU.add),
                                    r=[f"ps{ob}", "esink"], w=["den"])
                                P.op("dve", lambda e: e.reciprocal(out=den[:, :], in_=den[:, :]), r=["den"], w=["den"])
                                for r_ in range(4):
                                    h = 4 * g + r_
                                    P.op("dve", lambda e, r_=r_, h=h, ob=ob: e.tensor_scalar(
                                        out=otok[:, h * 64:(h + 1) * 64], in0=psf[:, ob, r_ * 128:r_ * 128 + 64], scalar1=den[:, r_:r_ + 1],
                                        scalar2=None, op0=ALU.mult),
                                        r=[f"ps{ob}", "den"], w=[f"otok{h // 2}"])
                            for kc in range(NKC):
                                P.op("pe", lambda e, kc=kc: e.transpose(psb[:, kc * 128:(kc + 1) * 128], otok[:, kc * 128:(kc + 1) * 128], identB[:]),
                                     r=[f"otok{kc}", "identB"], w=["psb"])
                            P.op("act", lambda e, q0=q0: e.activation(out=oT[:, :, q0:q0 + 128], in_=psb[:, :].rearrange("p (k n) -> p k n", n=128),
                                                                     func=AF.Copy),
                                 r=["psb"], w=kall("oT", q0, q0 + 128))
                        if sti == 0:
                            ck(10)
                        for blk in range(2):
                            s_, v_ = wq.get(3 + blk)
                            for jc in range(4):
                                c = blk * 4 + jc
                                for si, (a, b) in enumerate(osubs):
                                    T = b - a
                                    bk = bank()
                                    mm_group(psf[:, bk, 0:T], [(v_[:, kc, jc * 128:(jc + 1) * 128], nb[:, kc, a:b]) for kc in range(NKC)],
                                             [[f"ring{s_}"] + kk("nb", kc, a, b) for kc in range(NKC)], f"ps{bk}")
                                    gi = state.get("gi", 0)
                                    state["gi"] = gi + 1
                                    t1 = gt[:, gi % 2, 0, 0:T]
                                    t2 = gt[:, gi % 2, 1, 0:T]
                                    k1, k2 = f"gt{gi % 2}a", f"gt{gi % 2}b"
                                    P.op("act", lambda e, t1=t1, bk=bk, T=T: e.activation(out=t1, in_=psf[:, bk, 0:T], func=AF.Square), r=[f"ps{bk}"], w=[k1])
                                    P.op("dve", lambda e, t1=t1: e.tensor_scalar(out=t1, in0=t1, scalar1=0.044715 * 0.7978845608028654,
                                                                                scalar2=0.7978845608028654, op0=ALU.mult, op1=ALU.add),
                                         r=[k1], w=[k1])
                                    P.op("dve", lambda e, t1=t1, bk=bk, T=T: e.tensor_tensor(out=t1, in0=t1, in1=psf[:, bk, 0:T], op=ALU.mult),
                                         r=[k1, f"ps{bk}"], w=[k1])
                                    P.op("act", lambda e, t1=t1, t2=t2: e.activation(out=t2, in_=t1, func=AF.Tanh), r=[k1], w=[k2])
                                    P.op("dve", lambda e, t2=t2, bk=bk, T=T, c=c, a=a, b=b: e.scalar_tensor_tensor(
                                        out=qT[:, c, a:b], in0=t2, scalar=1.0, in1=psf[:, bk, 0:T], op0=ALU.add, op1=ALU.mult),
                                        r=[k2, f"ps{bk}"], w=kk("qT", c, a, b))
                        if sti == 0:
                            ck(11)
                        o0 = g0 + own0 - HALO
                        for c in range(NKC):
                            ab = c % 2
                            P.op("sp", lambda e, c=c, ab=ab, o0=o0, So=So: e.dma_start(out=abuf[:, ab, 0:So], in_=a_scr[:, c, o0:o0 + So]),
                                 r=[f"ascr{c}:{sti}"], w=[f"abuf{ab}"], dma=f"ald{ab}")
                            P.op("sp", lambda e, c=c, ab=ab, o0=o0, So=So: e.dma_start(out=bbuf[:, ab, 0:So], in_=b_scr[:, c, o0:o0 + So]),
                                 r=[f"bscr{c}:{sti}"], w=[f"bbuf{ab}"], dma=f"bld{ab}")
                            P.op("dve", lambda e, c=c, ab=ab, So=So: e.tensor_tensor_scan(
                                out=hrt[:, 0:So], data0=abuf[:, ab, 0:So], data1=bbuf[:, ab, 0:So], initial=carry2[:, c:c + 1],
                                op0=ALU.mult, op1=ALU.add),
                                r=[f"abuf{ab}", f"bbuf{ab}", f"carry2_{c}", "carry2"], w=["hrt"])
                            P.op("dve", lambda e, c=c, So=So: e.tensor_copy(out=carry2[:, c:c + 1], in_=hrt[:, So - 1:So]),
                                 r=["hrt", "carry2"], w=[f"carry2_{c}"])
                            P.op("dve", lambda e, c=c, So=So, own0=own0, S=S: e.scalar_tensor_tensor(
                                out=qT[:, c, own0:S], in0=hrt[:, 0:So], scalar=0.5, in1=qT[:, c, own0:S], op0=ALU.mult, op1=ALU.mult),
                                r=["hrt"] + kk("qT", c, own0, S), w=kk("qT", c, own0, S))
                        if sti == 0:
                            ck(12)
                        for mb in range(2):
                            base = 5 + 4 * mb
                            sA = wq.get(base, base); sB = wq.get(base + 1, base); sL = wq.get(base + 2, base); sO = wq.get(base + 3, base)
                            for mc in range(4):
                                m = mb * 4 + mc
                                for (a, b) in osubs:
                                    T = b - a
                                    bA, bB, bL, bO = bank(), bank(), bank(), bank()
                                    for (bk_, (s_, v_), src, sname) in ((bA, sA, nb, "nb"), (bB, sB, nb, "nb"), (bL, sL, qT, "qT"), (bO, sO, oT, "oT")):
                                        mm_group(psf[:, bk_, 0:T], [(v_[:, kc, mc * 128:(mc + 1) * 128], src[:, kc, a:b]) for kc in range(NKC)],
                                                 [[f"ring{s_}"] + kk(sname, kc, a, b) for kc in range(NKC)], f"ps{bk_}")
                                    gi = state.get("gi", 0)
                                    state["gi"] = gi + 1
                                    t1 = gt[:, gi % 2, 0, 0:T]
                                    t2 = gt[:, gi % 2, 1, 0:T]
                                    k1, k2 = f"gt{gi % 2}a", f"gt{gi % 2}b"
                                    P.op("act", lambda e, t1=t1, bA=bA, T=T, m=m: e.activation(out=t1, in_=psf[:, bA, 0:T], func=AF.Sigmoid,
                                                                                           bias=V("b_gate_a", m)), r=[f"ps{bA}", "vecs"], w=[k1])
                                    P.op("act", lambda e, t2=t2, bB=bB, T=T, m=m: e.activation(out=t2, in_=psf[:, bB, 0:T], func=AF.Sigmoid,
                                                                                           bias=V("b_gate_b", m)), r=[f"ps{bB}", "vecs"], w=[k2])
                                    P.op("dve", lambda e, t1=t1, bL=bL, T=T: e.tensor_tensor(out=t1, in0=t1, in1=psf[:, bL, 0:T], op=ALU.mult),
                                         r=[k1, f"ps{bL}"], w=[k1])
                                    P.op("dve", lambda e, t2=t2, bO=bO, T=T: e.tensor_tensor(out=t2, in0=t2, in1=psf[:, bO, 0:T], op=ALU.mult),
                                         r=[k2, f"ps{bO}"], w=[k2])
                                    P.op("dve", lambda e, t1=t1, t2=t2, m=m, a=a, b=b: e.tensor_tensor(out=mg[:, m, a:b], in0=t1, in1=t2, op=ALU.add),
                                         r=[k1, k2], w=kk("mg", m, a, b))
                        if sti == 0:
                            ck(13)
                        P.barrier()
                        for mb in range(2):
                            s_, v_ = wq.get(13 + mb)
                            for mc in range(4):
                                m = mb * 4 + mc
                                for (a, b) in osubs:
                                    T = b - a
                                    bk = bank()
                                    mm_group(psf[:, bk, 0:T], [(v_[:, kc, mc * 128:(mc + 1) * 128], mg[:, kc, a:b]) for kc in range(NKC)],
                                             [[f"ring{s_}"] + kk("mg", kc, a, b) for kc in range(NKC)], f"ps{bk}")
                                    P.op("dve", lambda e, m=m, a=a, b=b, bk=bk, T=T: e.tensor_copy(out=fT[:, m, a:b], in_=psf[:, bk, 0:T]),
                                         r=[f"ps{bk}"], w=kk("fT", m, a, b))
                                    P.op("act", lambda e, m=m, a=a, b=b: e.activation(out=nb[:, m, a:b], in_=fT[:, m, a:b], func=AF.Square),
                                         r=kk("fT", m, a, b), w=kk("nb", m, a, b))
                        post_residual(xT, nb, fT, vecs, VIDX["mix_post_g"], own0, S)
                    P.barrier()
                    with ExitStack() as s2:
                        ostage = sb("ostage", [128, 2, D], F32, s2)
                        h1 = sb("h1B", [128, NFC, SMAX], BF16, s2)
                        sg = sb("sgB", [128, 2, 512], F32, s2)
                        ffn(xT, nb, fT, h1, sg, w1b, w3b, w2b, "ffn2_pre_g", 8, own0, S)
                        for t in range(So // 128):
                            c0_ = own0 + t * 128
                            ti = state.get("oi", 0)
                            state["oi"] = ti + 1
                            ob_ = ti % 2
                            for hb in range(2):
                                bk = bank()
                                for q in range(4):
                                    kc = hb * 4 + q
                                    P.op("pe", lambda e, bk=bk, q=q, kc=kc, c0_=c0_: e.transpose(
                                        psf[:, bk, q * 128:(q + 1) * 128], xT[:, kc, c0_:c0_ + 128], identF[:]),
                                        r=kk("xT", kc, c0_, c0_ + 128) + ["identF"], w=[f"ps{bk}"])
                                ov = ostage[:, ob_, hb * 512:(hb + 1) * 512]
                                if hb == 0:
                                    P.op("act", lambda e, ov=ov, bk=bk: e.activation(out=ov, in_=psf[:, bk, :], func=AF.Copy),
                                         r=[f"ps{bk}"], w=[f"ost{ob_}:{hb}"])
                                else:
                                    P.op("dve", lambda e, ov=ov, bk=bk: e.tensor_copy(out=ov, in_=psf[:, bk, :]),
                                         r=[f"ps{bk}"], w=[f"ost{ob_}:{hb}"])
                            r0 = g0 + c0_ - HALO
                            P.op("sp", lambda e, ob_=ob_, r0=r0: e.dma_start(out=y_out[r0:r0 + 128, :], in_=ostage[:, ob_, :]),
                                 r=[f"ost{ob_}:0", f"ost{ob_}:1"], w=[f"y{r0}"], dma=f"yst{ob_}")
                    P.barrier()
        except _Stop:
            pass
        nops, nwait = P.emit()
        print(f"[kernel] ops={nops} waits={nwait}")
    return nc


def _t5_bucket_np(rel):
    nb = 16
    me = 8
    ret = np.where(rel > 0, nb, 0)
    n = np.abs(rel)
    nf = np.maximum(n, 1).astype(np.float32)
    large = me + (np.log(nf / np.float32(me)) / np.float32(math.log(128 / me)) * np.float32(nb - me)).astype(np.int32)
    large = np.minimum(large, nb - 1)
    return ret + np.where(n < me, n, large)


def _bias_tables(rel_bias):
    i = np.arange(128)[None, :]
    j = np.arange(256)[:, None]
    rel = (j - 128) - i
    buckets = _t5_bucket_np(rel)
    ci = i // 64
    kcr = j // 64
    valid = (kcr >= ci) & (kcr <= ci + 2)
    tab = rel_bias[buckets]
    tab = np.where(valid[:, :, None], tab, np.float32(NEG)).astype(np.float32)
    tab = tab.reshape(2, 128, 128, 16)
    return np.ascontiguousarray(np.transpose(tab, (1, 3, 0, 2)))


_NC_CACHE = {}


def kernel(**inputs):
    f32 = np.float32
    x = np.asarray(inputs["x"], f32)
    B, S, _ = x.shape

    def vec8(v):
        return np.asarray(v, f32).reshape(8, 128).T

    cw = np.asarray(inputs["conv_w"], f32)[0]
    bg = np.asarray(inputs["b_gate"], f32)[0]
    vd = {
        "ffn1_pre_g": inputs["ffn1_pre_g"][0], "ffn1_post_g": inputs["ffn1_post_g"][0], "mix_pre_g": inputs["mix_pre_g"][0],
        "conv_w0": cw[0], "conv_w1": cw[1], "conv_w2": cw[2], "conv_w3": cw[3], "conv_b": inputs["conv_b"][0],
        "rg_a_b": inputs["rg_a_b"][0], "rg_x_b": inputs["rg_x_b"][0], "lru_lambda": inputs["lru_lambda"][0],
        "b_gate_a": bg[:D], "b_gate_b": bg[D:], "mix_post_g": inputs["mix_post_g"][0],
        "ffn2_pre_g": inputs["ffn2_pre_g"][0], "ffn2_post_g": inputs["ffn2_post_g"][0],
    }
    vecs = np.ascontiguousarray(np.concatenate([vec8(vd[n]) for n in VEC_NAMES], axis=1))
    rgbd = np.zeros((128, 2, 8, 128), f32)
    for gi, nm in enumerate(("rg_a_w", "rg_x_w")):
        wv = np.asarray(inputs[nm], f32)[0]
        for c in range(8):
            rgbd[0:64, gi, c, 0:64] = wv[2 * c]
            rgbd[64:128, gi, c, 64:128] = wv[2 * c + 1]
    rgbd = rgbd.reshape(128, -1)
    biasT = _bias_tables(np.asarray(inputs["rel_bias"], f32))
    sinks_b = np.ascontiguousarray(np.broadcast_to(np.asarray(inputs["attn_sinks"], f32)[0][None, :], (128, 16)))
    ident = np.eye(128, dtype=f32)
    common = {
        "ffn1_w1": np.asarray(inputs["ffn1_w1"], f32)[0], "ffn1_w3": np.asarray(inputs["ffn1_w3"], f32)[0],
        "ffn1_w2": np.asarray(inputs["ffn1_w2"], f32)[0], "ffn2_w1": np.asarray(inputs["ffn2_w1"], f32)[0],
        "ffn2_w3": np.asarray(inputs["ffn2_w3"], f32)[0], "ffn2_w2": np.asarray(inputs["ffn2_w2"], f32)[0],
        "w_in": np.asarray(inputs["w_in"], f32)[0], "w_lru_out": np.asarray(inputs["w_lru_out"], f32)[0],
        "w_attn_out": np.asarray(inputs["w_attn_out"], f32)[0], "w_gate": np.asarray(inputs["w_gate"], f32)[0],
        "w_o": np.asarray(inputs["w_o"], f32)[0], "vecs": vecs, "rgbd": rgbd,
        "biasT": biasT.reshape(128, -1), "sinks_b": sinks_b, "ident": ident,
    }
    if DEBUG_SMALL:
        for nm in SMALL_NAMES:
            common[nm] = np.zeros((128, 128), f32)
    in_maps = []
    for core in range(8):
        b, half = core // 2, core % 2
        own = x[b, half * NOWN:(half + 1) * NOWN]
        sel = np.zeros((128, 8), f32)
        if half == 0:
            prev = np.zeros((NPREV, D), f32)
            hmask = np.full((128, 1), NEG, f32)
            flag = np.zeros((128, 1), f32)
        else:
            prev = x[b, 0:NPREV]
            hmask = np.zeros((128, 1), f32)
            flag = np.ones((128, 1), f32)
        m = dict(common)
        m["xs"] = np.ascontiguousarray(np.concatenate([prev, own], axis=0))
        m["flag"] = flag
        m["hmask"] = hmask
        m["sel"] = sel
        in_maps.append(m)
    if "nc" not in _NC_CACHE:
        _NC_CACHE["nc"] = build_program()
    res = run_bass_kernel_spmd(_NC_CACHE["nc"], in_maps, core_ids=list(range(8)))
    out = np.empty((B, S, D), f32)
    for core in range(8):
        b, half = core // 2, core % 2
        out[b, half * NOWN:(half + 1) * NOWN] = res.results[core]["y"]
    return out
```
